# Optimizing a Trainium2 kernel written in Bass

```python
import jax
import jax.numpy as jnp
from jax import lax
import numpy as np

D_MODEL = 4096
BATCH = 1
SEQ = 8192
DEPTH = 4

HEAD_DIM = 128
N_HEADS = D_MODEL // 512
W_MIX = N_HEADS * HEAD_DIM
N_BRANCH = 3
N_IN_GROUPS = 10
D_FF = D_MODEL
CONV_K = 4
Q_BLOCK = 128
CHUNK = 64
EPS = 1e-6
NEG_BIG = -1e30
LB_FLOOR = 1e-30

kernel_name = 'hybrid_stickbreak_hgrn2_mlstm_macaron'


def rmsnorm(x, gain, eps=EPS):
    xf = x.astype(jnp.float32)
    y = xf * lax.rsqrt(jnp.mean(xf * xf, axis=-1, keepdims=True) + eps)
    return (y * gain.astype(jnp.float32)).astype(x.dtype)


def swiglu(h, w_gate, w_up, w_down):
    return (jax.nn.silu(h @ w_gate) * (h @ w_up)) @ w_down


def split_heads(a):
    b, s, _ = a.shape
    return a.reshape(b, s, N_HEADS, HEAD_DIM).transpose(0, 2, 1, 3)


def merge_heads(a):
    b, h, s, d = a.shape
    return a.transpose(0, 2, 1, 3).reshape(b, s, h * d)


def to_chunks(a):
    b, h, s = a.shape[:3]
    a = a.reshape((b, h, s // CHUNK, CHUNK) + a.shape[3:])
    return jnp.moveaxis(a, 2, 0)


def from_chunks(a):
    a = jnp.moveaxis(a, 0, 2)
    b, h, n, l = a.shape[:4]
    return a.reshape((b, h, n * l) + a.shape[4:])


def stick_breaking_attention(q, k, v):
    b, h, s, d = q.shape
    n_blk = s // Q_BLOCK
    scale = d ** -0.5
    q_blocks = jnp.moveaxis(q.reshape(b, h, n_blk, Q_BLOCK, d), 2, 0)
    key_pos = jnp.arange(s)

    def one_block(args):
        blk, qb = args
        z = jnp.einsum('bhtd,bhsd->bhts', qb, k) * scale
        q_pos = blk * Q_BLOCK + jnp.arange(Q_BLOCK)
        mask = key_pos[None, :] < q_pos[:, None]
        log_1m = jnp.where(mask, jax.nn.log_sigmoid(-z), 0.0)
        between = lax.cumsum(log_1m, axis=3, reverse=True) - log_1m
        log_att = jnp.where(mask, jax.nn.log_sigmoid(z) + between, NEG_BIG)
        att = jnp.exp(log_att)
        return jnp.einsum('bhts,bhsd->bhtd', att, v)

    out = lax.map(one_block, (jnp.arange(n_blk), q_blocks))
    return jnp.moveaxis(out, 0, 2).reshape(b, h, s, d)


def hgrn2_recurrence(q, k, v, log_f):
    b, h, s, dk = q.shape
    mask = jnp.tril(jnp.ones((CHUNK, CHUNK), dtype=bool))

    def step(state, inp):
        qc, kc, vc, lf = inp
        cum = jnp.cumsum(lf, axis=2)
        diff = jnp.where(mask[:, :, None],
                         cum[:, :, :, None, :] - cum[:, :, None, :, :], NEG_BIG)
        att = jnp.einsum('bhtd,bhtsd,bhsd->bhts', qc, jnp.exp(diff), kc)
        last = cum[:, :, -1, :]
        out = att @ vc + jnp.einsum('bhtd,bhdv->bhtv', qc * jnp.exp(cum), state)
        new_state = (jnp.exp(last)[..., None] * state
                     + jnp.einsum('bhsd,bhsv->bhdv', kc * jnp.exp(last[:, :, None, :] - cum), vc))
        return new_state, out

    init = jnp.zeros((b, h, dk, v.shape[-1]), jnp.float32)
    _, out = lax.scan(step, init, (to_chunks(q), to_chunks(k), to_chunks(v), to_chunks(log_f)))
    return from_chunks(out)


def mlstm_recurrence(q, k, v, log_i, log_f):
    b, h, s, d = q.shape
    mask = jnp.tril(jnp.ones((CHUNK, CHUNK), dtype=bool))

    def step(carry, inp):
        c_mem, n_mem, m = carry
        qc, kc, vc, li, lf = inp
        cum = jnp.cumsum(lf, axis=-1)
        dmat = jnp.where(mask, cum[..., :, None] - cum[..., None, :] + li[..., None, :], NEG_BIG)
        inter = cum + m[..., None]
        m_t = jnp.maximum(inter, jnp.max(dmat, axis=-1))
        w = jnp.exp(dmat - m_t[..., None]) * jnp.einsum('bhtd,bhsd->bhts', qc, kc)
        carry_w = jnp.exp(inter - m_t)
        num = w @ vc + carry_w[..., None] * jnp.einsum('bhtd,bhdv->bhtv', qc, c_mem)
        den = jnp.sum(w, axis=-1) + carry_w * jnp.einsum('bhtd,bhd->bht', qc, n_mem)
        h_out = num / jnp.maximum(jnp.abs(den), jnp.exp(-m_t))[..., None]
        last = cum[..., -1]
        g = last[..., None] - cum + li
        m_new = jnp.maximum(last + m, jnp.max(g, axis=-1))
        wk = jnp.exp(g - m_new[..., None])
        decay = jnp.exp(last + m - m_new)
        c_new = decay[..., None, None] * c_mem + jnp.einsum('bhs,bhsd,bhsv->bhdv', wk, kc, vc)
        n_new = decay[..., None] * n_mem + jnp.einsum('bhs,bhsd->bhd', wk, kc)
        return (c_new, n_new, m_new), h_out

    init = (jnp.zeros((b, h, d, v.shape[-1]), jnp.float32),
            jnp.zeros((b, h, d), jnp.float32),
            jnp.zeros((b, h), jnp.float32))
    _, out = lax.scan(step, init, (to_chunks(q), to_chunks(k), to_chunks(v),
                                   to_chunks(log_i), to_chunks(log_f)))
    return from_chunks(out)


def causal_short_conv(x, w, bias):
    y = lax.conv_general_dilated(x, w[:, None, :].astype(x.dtype), window_strides=(1,),
                                 padding=[(CONV_K - 1, 0)],
                                 dimension_numbers=('NWC', 'WIO', 'NWC'),
                                 feature_group_count=x.shape[-1])
    return y + bias


def hybrid_mixer(h, lower_bound, w_in, sb_q_gain, sb_k_gain, hg_out_gain, ml_conv_w, ml_conv_b,
                 ml_w_q, ml_w_k, ml_w_if, ml_b_if, ml_out_gain, ml_skip, w_merge_gate,
                 w_branch_a, w_branch_b, w_branch_c, w_out):
    f32 = jnp.float32
    b, s, _ = h.shape
    proj = h @ w_in
    (sb_q, sb_k, sb_v, hg_f, hg_i, hg_q, hg_g, ml_x, ml_v, ml_z) = jnp.split(proj, N_IN_GROUPS, axis=-1)

    q_a = rmsnorm(split_heads(sb_q).astype(f32), sb_q_gain)
    k_a = rmsnorm(split_heads(sb_k).astype(f32), sb_k_gain)
    y_a = merge_heads(stick_breaking_attention(q_a, k_a, split_heads(sb_v).astype(f32))).astype(h.dtype)

    lb = lower_bound.astype(f32).reshape(N_HEADS, 1, HEAD_DIM)
    f_pre = split_heads(hg_f).astype(f32)
    log_f_b = jnp.logaddexp(jnp.log(jnp.maximum(lb, LB_FLOOR)),
                            jnp.log1p(-lb) + jax.nn.log_sigmoid(f_pre))
    k_b = (1.0 - lb) * jax.nn.sigmoid(-f_pre)
    o_b = hgrn2_recurrence(jax.nn.silu(split_heads(hg_q).astype(f32)), k_b,
                           split_heads(hg_i).astype(f32), log_f_b)
    y_b = (merge_heads(rmsnorm(o_b, hg_out_gain)) * jax.nn.silu(hg_g.astype(f32))).astype(h.dtype)

    xc_act = jax.nn.silu(causal_short_conv(ml_x, ml_conv_w, ml_conv_b))
    xh = xc_act.reshape(b, s, N_HEADS, HEAD_DIM)
    q_c = jnp.einsum('bshd,hde->bshe', xh, ml_w_q)
    k_c = jnp.einsum('bshd,hde->bshe', xh, ml_w_k)
    gates = jnp.concatenate([q_c.reshape(b, s, W_MIX), k_c.reshape(b, s, W_MIX), ml_v], axis=-1) @ ml_w_if + ml_b_if
    gates = gates.astype(f32).transpose(0, 2, 1)
    log_i_c = gates[:, :N_HEADS]
    log_f_c = jax.nn.log_sigmoid(gates[:, N_HEADS:])
    h_c = mlstm_recurrence(q_c.transpose(0, 2, 1, 3).astype(f32),
                           k_c.transpose(0, 2, 1, 3).astype(f32) * HEAD_DIM ** -0.5,
                           split_heads(ml_v).astype(f32), log_i_c, log_f_c)
    y_c = ((merge_heads(rmsnorm(h_c, ml_out_gain)) + ml_skip * xc_act) * jax.nn.silu(ml_z)).astype(h.dtype)

    branches = (y_a @ w_branch_a, y_b @ w_branch_b, y_c @ w_branch_c)
    merged = sum(jax.nn.sigmoid(h @ w_merge_gate[i]) * branches[i] for i in range(N_BRANCH))
    return merged @ w_out


def setup_inputs(seed: int = 0) -> dict:
    key = jax.random.key(seed)
    ks = jax.random.split(key, 32)
    L, D, F, W, H = DEPTH, D_MODEL, D_FF, W_MIX, N_HEADS

    def nrm(i, shape, scale):
        return jax.random.normal(ks[i], shape, jnp.float32) * scale

    def gain(i, shape):
        return 1.0 + nrm(i, shape, 0.02)

    ml_b_if = jnp.concatenate([nrm(16, (L, H), 0.1),
                               jnp.linspace(3.0, 6.0, H, dtype=jnp.float32)[None, :] + nrm(17, (L, H), 0.1)], axis=1)
    return {
        'x': nrm(0, (BATCH, SEQ, D), 1.0),
        'ffn1_norm': gain(1, (L, D)),
        'ffn1_w_gate': nrm(2, (L, D, F), D ** -0.5),
        'ffn1_w_up': nrm(3, (L, D, F), D ** -0.5),
        'ffn1_w_down': nrm(4, (L, F, D), F ** -0.5),
        'mix_norm': gain(5, (L, D)),
        'w_in': nrm(6, (L, D, N_IN_GROUPS * W), D ** -0.5),
        'sb_q_gain': gain(7, (L, HEAD_DIM)),
        'sb_k_gain': gain(8, (L, HEAD_DIM)),
        'hg_lb_logits': nrm(9, (L, W), 0.5),
        'hg_out_gain': gain(10, (L, HEAD_DIM)),
        'ml_conv_w': nrm(11, (L, CONV_K, W), CONV_K ** -0.5),
        'ml_conv_b': nrm(12, (L, W), 0.01),
        'ml_w_q': nrm(13, (L, H, HEAD_DIM, HEAD_DIM), HEAD_DIM ** -0.5),
        'ml_w_k': nrm(14, (L, H, HEAD_DIM, HEAD_DIM), HEAD_DIM ** -0.5),
        'ml_w_if': nrm(15, (L, 3 * W, 2 * H), (3 * W) ** -0.5),
        'ml_b_if': ml_b_if,
        'ml_out_gain': gain(18, (L, HEAD_DIM)),
        'ml_skip': gain(19, (L, W)),
        'w_merge_gate': nrm(20, (L, N_BRANCH, D, D), D ** -0.5),
        'w_branch_a': nrm(21, (L, W, D), W ** -0.5),
        'w_branch_b': nrm(22, (L, W, D), W ** -0.5),
        'w_branch_c': nrm(23, (L, W, D), W ** -0.5),
        'w_out': nrm(24, (L, D, D), D ** -0.5),
        'ffn2_norm': gain(25, (L, D)),
        'ffn2_w_gate': nrm(26, (L, D, F), D ** -0.5),
        'ffn2_w_up': nrm(27, (L, D, F), D ** -0.5),
        'ffn2_w_down': nrm(28, (L, F, D), F ** -0.5),
    }


def reference(x, ffn1_norm, ffn1_w_gate, ffn1_w_up, ffn1_w_down, mix_norm, w_in, sb_q_gain, sb_k_gain,
              hg_lb_logits, hg_out_gain, ml_conv_w, ml_conv_b, ml_w_q, ml_w_k, ml_w_if, ml_b_if,
              ml_out_gain, ml_skip, w_merge_gate, w_branch_a, w_branch_b, w_branch_c, w_out,
              ffn2_norm, ffn2_w_gate, ffn2_w_up, ffn2_w_down):
    p = jax.nn.softmax(hg_lb_logits.astype(jnp.float32), axis=0)
    lower_bounds = jnp.cumsum(p, axis=0) - p[0]
    for l in range(DEPTH):
        x = x + 0.5 * swiglu(rmsnorm(x, ffn1_norm[l]), ffn1_w_gate[l], ffn1_w_up[l], ffn1_w_down[l])
        x = x + hybrid_mixer(rmsnorm(x, mix_norm[l]), lower_bounds[l], w_in[l], sb_q_gain[l], sb_k_gain[l],
                             hg_out_gain[l], ml_conv_w[l], ml_conv_b[l], ml_w_q[l], ml_w_k[l], ml_w_if[l],
                             ml_b_if[l], ml_out_gain[l], ml_skip[l], w_merge_gate[l], w_branch_a[l],
                             w_branch_b[l], w_branch_c[l], w_out[l])
        x = x + 0.5 * swiglu(rmsnorm(x, ffn2_norm[l]), ffn2_w_gate[l], ffn2_w_up[l], ffn2_w_down[l])
    return x
```

```python
import numpy as np
import concourse.bass as bass
import concourse.mybir as mybir
from concourse.bass_utils import run_bass_kernel_spmd

F32 = mybir.dt.float32
BF16 = mybir.dt.bfloat16
AF = mybir.ActivationFunctionType
ALU = mybir.AluOpType
AX = mybir.AxisListType

NCORES = 8
D = 4096
S = 8192
DEPTH = 4
KC = D // 128
TOK = S // NCORES
W_MIX = 1024
HD = 128
EPS = 1e-6


class Tok:
    __slots__ = ("last_w", "readers", "dma_readers")

    def __init__(self):
        self.last_w = None
        self.readers = {}
        self.dma_readers = []


class Op:
    __slots__ = ("eng", "fn", "deps", "signal", "count", "is_dma", "dsem", "dval")


class Prog:
    ENGS = ("pe", "act", "dve", "pool", "sp")
    NDMASEM = 8

    def __init__(self, nc):
        self.nc = nc
        self.ops = {e: [] for e in self.ENGS}
        self.ndma = {e: 0 for e in self.ENGS}
        self.dma_hist = {e: [] for e in self.ENGS}

    def tok(self):
        return Tok()

    def toks(self, n):
        return [Tok() for _ in range(n)]

    def emit(self, eng, fn, reads=(), writes=(), dma=False):
        op = Op()
        op.eng, op.fn, op.signal, op.is_dma = eng, fn, False, dma
        op.count = op.dsem = op.dval = None
        deps = []
        for t in reads:
            if t.last_w is not None:
                deps.append(t.last_w)
        for t in writes:
            if t.last_w is not None:
                deps.append(t.last_w)
            deps.extend(t.readers.values())
            deps.extend(t.dma_readers)
        if dma:
            n = self.ndma[eng]
            self.ndma[eng] = n + 1
            op.dsem = (eng, n % self.NDMASEM)
            op.dval = 16 * (n // self.NDMASEM + 1)
            hist = self.dma_hist[eng]
            if n >= self.NDMASEM:
                deps.append(hist[n - self.NDMASEM])
            hist.append(op)
        seen = set()
        op.deps = []
        for d in deps:
            if d is op or id(d) in seen:
                continue
            seen.add(id(d))
            if (not d.is_dma) and d.eng == eng and eng == "pe" and not dma:
                continue
            if not d.is_dma:
                d.signal = True
            op.deps.append(d)
        for t in reads:
            if dma:
                t.dma_readers.append(op)
            else:
                t.readers[eng] = op
        for t in writes:
            t.last_w = op
            t.readers = {}
            t.dma_readers = []
        self.ops[eng].append(op)
        return op

    def pe(self, fn, reads=(), writes=()):
        return self.emit("pe", fn, reads, writes)

    def act(self, fn, reads=(), writes=()):
        return self.emit("act", fn, reads, writes)

    def dve(self, fn, reads=(), writes=()):
        return self.emit("dve", fn, reads, writes)

    def pool(self, fn, reads=(), writes=()):
        return self.emit("pool", fn, reads, writes)

    def dma(self, q, out, in_, reads=(), writes=()):
        return self.emit(q, lambda e: e.dma_start(out=out, in_=in_), reads, writes, dma=True)

    def finish(self):
        nc = self.nc
        for e in self.ENGS:
            c = 0
            for op in self.ops[e]:
                if op.signal and not op.is_dma:
                    c += 1
                    op.count = c
        import contextlib

        with contextlib.ExitStack() as st:
            csem = {e: st.enter_context(nc.semaphore("c_" + e)) for e in self.ENGS}
            dsem = {}
            for e in self.ENGS:
                if self.ndma[e]:
                    for j in range(min(self.NDMASEM, self.ndma[e])):
                        dsem[(e, j)] = st.enter_context(nc.semaphore("d_%s%d" % (e, j)))
            block = st.enter_context(nc.Block())

            def semval(d):
                if d.is_dma:
                    return dsem[d.dsem], d.dval
                return csem[d.eng], d.count

            def run(e, eng):
                waited = {}
                ops = self.ops[e]
                for op in ops:
                    for d in op.deps:
                        sm, v = semval(d)
                        k = id(sm)
                        if waited.get(k, 0) < v:
                            eng.wait_ge(sm, v)
                            waited[k] = v
                    ins = op.fn(eng)
                    if op.is_dma:
                        ins.then_inc(dsem[op.dsem], 16)
                    elif op.signal:
                        ins.then_inc(csem[e], 1)
                if e == "sp":
                    for q in self.ENGS:
                        for d in self.dma_hist[q][-self.NDMASEM:]:
                            sm, v = semval(d)
                            eng.wait_ge(sm, v)

            @block.tensor
            def _(eng):
                run("pe", eng)

            @block.scalar
            def _(eng):
                run("act", eng)

            @block.vector
            def _(eng):
                run("dve", eng)

            @block.gpsimd
            def _(eng):
                run("pool", eng)

            @block.sync
            def _(eng):
                run("sp", eng)


TT = 512
WCOLS = 256


def rms_to_hT(P, R, xT_d, gain_sb, gain_tok, hT, hT_toks, t0, ps, ps_tok, width=TT):
    xs, xtk, sq, sqtk, rstd, rstd_tok = R["xs"], R["xtk"], R["sq"], R["sqtk"], R["rstd"], R["rstd_tok"]
    ones_bf, ones_tok = R["ones"], R["ones_tok"]
    for c in range(KC):
        b = c % 2
        P.dma("sp", xs[b][:], xT_d[:, c, t0:t0 + width], writes=[xtk[b]])
        P.act(lambda e, b=b: e.activation(out=sq[b][:], in_=xs[b][:], func=AF.Square),
              reads=[xtk[b]], writes=[sqtk[b]])
        P.pe(lambda e, b=b, c=c: e.matmul(ps[:], lhsT=ones_bf[:], rhs=sq[b][:], start=(c == 0), stop=(c == KC - 1)),
             reads=[sqtk[b], ones_tok], writes=[ps_tok])
    P.dve(lambda e: e.tensor_scalar(out=rstd[:], in0=ps[:], scalar1=1.0 / D, scalar2=EPS, op0=ALU.mult, op1=ALU.add),
          reads=[ps_tok], writes=[rstd_tok])
    P.act(lambda e: e.activation(out=rstd[:], in_=rstd[:], func=AF.Sqrt), reads=[rstd_tok], writes=[rstd_tok])
    P.dve(lambda e: e.reciprocal(out=rstd[:], in_=rstd[:]), reads=[rstd_tok], writes=[rstd_tok])
    for c in range(KC):
        b = c % 2
        P.dma("sp", xs[b][:], xT_d[:, c, t0:t0 + width], writes=[xtk[b]])
        P.dve(lambda e, b=b, c=c: e.scalar_tensor_tensor(out=hT[:, c, :], in0=xs[b][:], scalar=gain_sb[:, c:c + 1],
                                                         in1=rstd[:], op0=ALU.mult, op1=ALU.mult),
              reads=[xtk[b], rstd_tok, gain_tok], writes=[hT_toks[c]])


def rms_resources(P, sb):
    R = {}
    R["xs"] = [sb("rx%d" % i, [128, TT], F32) for i in range(2)]
    R["xtk"] = P.toks(2)
    R["sq"] = [sb("rsq%d" % i, [128, TT], BF16) for i in range(2)]
    R["sqtk"] = P.toks(2)
    R["rstd"] = sb("rstd", [128, TT], F32)
    R["rstd_tok"] = P.tok()
    R["ones"] = sb("ones", [128, 128], BF16)
    R["ones_tok"] = P.tok()
    P.dve(lambda e: e.memset(R["ones"][:], 1.0), writes=[R["ones_tok"]])
    return R


def build_ffn():
    nc = bass.Bass("TRN2", target_bir_lowering=False)
    xT_d = nc.dram_tensor("xT", [128, KC, TOK], F32, kind="ExternalInput").ap()
    gain_d = nc.dram_tensor("gain", [128, KC], F32, kind="ExternalInput").ap()
    wg_d = nc.dram_tensor("wg", [D, D], F32, kind="ExternalInput").ap()
    wu_d = nc.dram_tensor("wu", [D, D], F32, kind="ExternalInput").ap()
    wd_d = nc.dram_tensor("wd", [D, D], F32, kind="ExternalInput").ap()
    oT_d = nc.dram_tensor("oT", [128, KC, TOK], F32, kind="ExternalOutput").ap()
    import contextlib

    P = Prog(nc)
    NT = TOK // TT
    with contextlib.ExitStack() as st:
        sb = lambda name, shape, dt: st.enter_context(nc.sbuf_tensor("s_" + name, shape, dt))
        hT = sb("hT", [128, KC, TOK], BF16)
        aT = sb("aT", [128, KC, TOK], BF16)
        hT_toks = [P.toks(KC) for _ in range(NT)]
        aT_toks = [P.toks(KC) for _ in range(NT)]
        NWB = 3
        wbuf = [sb("w%d" % i, [128, KC, WCOLS], BF16) for i in range(NWB)]
        wtok = P.toks(NWB)
        gain_sb = sb("gain_sb", [128, KC], F32)
        gain_tok = P.tok()
        R = rms_resources(P, sb)
        sl = [sb("sl%d" % i, [128, TT], F32) for i in range(2)]
        sl_toks = P.toks(2)
        ot = [sb("ot%d" % i, [128, TT], F32) for i in range(2)]
        ot_toks = P.toks(2)
        xr = [sb("xr%d" % i, [128, TT], F32) for i in range(2)]
        xr_toks = P.toks(2)
        pb = [st.enter_context(nc.psum_tensor("pb%d" % i, [128, TT], F32)) for i in range(8)]
        pb_toks = P.toks(8)

        P.dma("sp", gain_sb[:], gain_d[:, :], writes=[gain_tok])

        wcount = [0]

        def load_w(w_d, j):
            i = wcount[0] % NWB
            wcount[0] += 1
            src = w_d[:, j * WCOLS:(j + 1) * WCOLS].rearrange("(c p) n -> p c n", p=128)
            for q in range(4):
                P.dma("pool", wbuf[i][:, q * 8:(q + 1) * 8, :], src[:, q * 8:(q + 1) * 8, :], writes=[wtok[i]])
            return i

        pcount = [0]

        def nextbank():
            i = pcount[0] % 8
            pcount[0] += 1
            return i

        for tt in range(NT):
            rms_to_hT(P, R, xT_d, gain_sb, gain_tok, hT[:, :, tt * TT:(tt + 1) * TT], hT_toks[tt], tt * TT, pb[0], pb_toks[0])
        ecount = 0
        for j in range(D // WCOLS):
            ig = load_w(wg_d, j)
            iu = load_w(wu_d, j)
            for sub in range(WCOLS // 128):
                fb = j * (WCOLS // 128) + sub
                for tt in range(NT):
                    tsl = slice(tt * TT, (tt + 1) * TT)
                    bg, bu = nextbank(), nextbank()
                    for (wi, bk) in ((ig, bg), (iu, bu)):
                        for c in range(KC):
                            P.pe(lambda e, wi=wi, bk=bk, c=c, sub=sub, tsl=tsl: e.matmul(
                                pb[bk][:], lhsT=wbuf[wi][:, c, sub * 128:(sub + 1) * 128], rhs=hT[:, c, tsl],
                                start=(c == 0), stop=(c == KC - 1)),
                                reads=[wtok[wi], hT_toks[tt][c]], writes=[pb_toks[bk]])
                    s = ecount % 2
                    ecount += 1
                    P.act(lambda e, s=s, bg=bg: e.activation(out=sl[s][:], in_=pb[bg][:], func=AF.Silu),
                          reads=[pb_toks[bg]], writes=[sl_toks[s]])
                    P.dve(lambda e, s=s, bu=bu, fb=fb, tsl=tsl: e.tensor_tensor(out=aT[:, fb, tsl], in0=pb[bu][:], in1=sl[s][:], op=ALU.mult),
                          reads=[pb_toks[bu], sl_toks[s]], writes=[aT_toks[tt][fb]])
        for j in range(D // WCOLS):
            iw = load_w(wd_d, j)
            for sub in range(WCOLS // 128):
                cb = j * (WCOLS // 128) + sub
                for tt in range(NT):
                    t0 = tt * TT
                    tsl = slice(t0, t0 + TT)
                    bk = nextbank()
                    for c in range(KC):
                        P.pe(lambda e, iw=iw, bk=bk, c=c, sub=sub, tsl=tsl: e.matmul(
                            pb[bk][:], lhsT=wbuf[iw][:, c, sub * 128:(sub + 1) * 128], rhs=aT[:, c, tsl],
                            start=(c == 0), stop=(c == KC - 1)),
                            reads=[wtok[iw], aT_toks[tt][c]], writes=[pb_toks[bk]])
                    s = ecount % 2
                    ecount += 1
                    P.dma("sp", xr[s][:], xT_d[:, cb, t0:t0 + TT], writes=[xr_toks[s]])
                    P.dve(lambda e, s=s, bk=bk: e.scalar_tensor_tensor(out=ot[s][:], in0=pb[bk][:], scalar=0.5, in1=xr[s][:],
                                                                     op0=ALU.mult, op1=ALU.add),
                          reads=[pb_toks[bk], xr_toks[s]], writes=[ot_toks[s]])
                    P.dma("sp", oT_d[:, cb, t0:t0 + TT], ot[s][:], reads=[ot_toks[s]])
        P.finish()
    return nc


_CACHE = {}


def get_prog(name, builder):
    if name not in _CACHE:
        _CACHE[name] = builder()
    return _CACHE[name]


def to_fm(x2d):
    T = x2d.shape[0]
    n = T // TOK
    a = x2d.reshape(n, TOK, KC, 128).transpose(0, 3, 2, 1)
    return [np.ascontiguousarray(a[i]) for i in range(n)]


def from_fm(lst):
    a = np.stack(lst, 0)
    return np.ascontiguousarray(a.transpose(0, 3, 2, 1)).reshape(-1, D)


def fm_vec(g):
    return np.ascontiguousarray(g.reshape(KC, 128).T)


def run_ffn(xT_list, gain, wg, wu, wd):
    nc = get_prog("ffn", build_ffn)
    g = fm_vec(gain)
    in_maps = [{"xT": xT_list[i], "gain": g, "wg": wg, "wu": wu, "wd": wd} for i in range(NCORES)]
    res = run_bass_kernel_spmd(nc, in_maps, core_ids=list(range(NCORES)))
    return [r["oT"] for r in res.results]


NBLK_IN = 80
HALO = 128


def build_p2():
    import contextlib
    nc = bass.Bass("TRN2", target_bir_lowering=False)
    xT_d = nc.dram_tensor("xT", [128, KC, TOK], F32, kind="ExternalInput").ap()
    xh_d = nc.dram_tensor("xhalo", [128, KC, HALO], F32, kind="ExternalInput").ap()
    gain_d = nc.dram_tensor("gain", [128, KC], F32, kind="ExternalInput").ap()
    win_d = nc.dram_tensor("w_in", [D, 10 * W_MIX], F32, kind="ExternalInput").ap()
    cw_d = nc.dram_tensor("conv_w", [128, 8, 4], F32, kind="ExternalInput").ap()
    cb_d = nc.dram_tensor("conv_b", [128, 8], F32, kind="ExternalInput").ap()
    wq_d = nc.dram_tensor("wq", [128, 8, 128], F32, kind="ExternalInput").ap()
    wk_d = nc.dram_tensor("wk", [128, 8, 128], F32, kind="ExternalInput").ap()
    wif_d = nc.dram_tensor("wif", [128, 24, 16], F32, kind="ExternalInput").ap()
    bif_d = nc.dram_tensor("bif", [16, 1], F32, kind="ExternalInput").ap()
    cm_d = nc.dram_tensor("cm", [96, 128, TOK], F32, kind="ExternalOutput").ap()
    g2_d = nc.dram_tensor("g2", [2, 16, TOK], F32, kind="ExternalOutput").ap()
    P = Prog(nc)
    with contextlib.ExitStack() as st:
        sb = lambda name, shape, dt: st.enter_context(nc.sbuf_tensor("s_" + name, shape, dt))
        hT = sb("hT", [128, KC, TT], BF16)
        hT_toks = P.toks(KC)
        NWB = 3
        wbuf = [sb("w%d" % i, [128, KC, WCOLS], BF16) for i in range(NWB)]
        wtok = P.toks(NWB)
        gain_sb = sb("gain_sb", [128, KC], F32)
        gain_tok = P.tok()
        R = rms_resources(P, sb)
        cw = sb("cw", [128, 8, 4], F32)
        cb = sb("cb", [128, 8], F32)
        wq = sb("wq", [128, 8, 128], BF16)
        wk = sb("wk", [128, 8, 128], BF16)
        wif = sb("wif", [128, 24, 16], BF16)
        bif = sb("bif", [16, 1], F32)
        const_tok = P.tok()
        mlx = sb("mlx", [128, 8, 3 + TT], F32)
        mlx_toks = P.toks(8)
        catT = sb("catT", [128, 24, TT], BF16)
        cat_toks = P.toks(24)
        ev = [sb("ev%d" % i, [128, TT], F32) for i in range(4)]
        ev_toks = P.toks(4)
        acc = [sb("acc%d" % i, [128, TT], F32) for i in range(2)]
        acc_toks = P.toks(2)
        gsb = sb("gsb", [16, TT], F32)
        gsb2 = sb("gsb2", [16, TT], F32)
        gtok, gtok2 = P.tok(), P.tok()
        pb = [st.enter_context(nc.psum_tensor("pb%d" % i, [128, TT], F32)) for i in range(8)]
        pb_toks = P.toks(8)

        P.dma("sp", gain_sb[:], gain_d[:, :], writes=[gain_tok])
        P.dma("sp", cw[:], cw_d[:, :, :], writes=[const_tok])
        P.dma("sp", cb[:], cb_d[:, :], writes=[const_tok])
        P.dma("sp", bif[:], bif_d[:, :], writes=[const_tok])
        P.dma("pool", wq[:], wq_d[:, :, :], writes=[const_tok])
        P.dma("pool", wk[:], wk_d[:, :, :], writes=[const_tok])
        P.dma("pool", wif[:], wif_d[:, :, :], writes=[const_tok])

        wcount, pcount, ecount = [0], [0], [0]

        def load_w(j, ncols=WCOLS):
            i = wcount[0] % NWB
            wcount[0] += 1
            src = win_d[:, j * WCOLS:j * WCOLS + ncols].rearrange("(c p) n -> p c n", p=128)
            for q in range(4):
                P.dma("pool", wbuf[i][:, q * 8:(q + 1) * 8, :ncols], src[:, q * 8:(q + 1) * 8, :], writes=[wtok[i]])
            return i

        def nextbank():
            i = pcount[0] % 8
            pcount[0] += 1
            return i

        def nextev():
            i = ecount[0] % 4
            ecount[0] += 1
            return i

        hTh = sb("hTh", [128, KC, HALO], BF16)
        hTh_toks = P.toks(KC)
        Rh = dict(R)
        Rh["xs"] = [R["xs"][i][:, :HALO] for i in range(2)]
        Rh["sq"] = [R["sq"][i][:, :HALO] for i in range(2)]
        Rh["rstd"] = R["rstd"][:, :HALO]
        rms_to_hT(P, Rh, xh_d, gain_sb, gain_tok, hTh, hTh_toks, 0, pb[0][:, :HALO], pb_toks[0], width=HALO)
        for j in range(4):
            iw = load_w(28 + j)
            for sub in range(2):
                hh = 2 * j + sub
                bk = nextbank()
                for c in range(KC):
                    P.pe(lambda e, iw=iw, bk=bk, c=c, sub=sub: e.matmul(
                        pb[bk][:, :HALO], lhsT=wbuf[iw][:, c, sub * 128:(sub + 1) * 128], rhs=hTh[:, c, :],
                        start=(c == 0), stop=(c == KC - 1)), reads=[wtok[iw], hTh_toks[c]], writes=[pb_toks[bk]])
                P.dve(lambda e, hh=hh, bk=bk: e.tensor_copy(out=mlx[:, hh, 0:3], in_=pb[bk][:, HALO - 3:HALO]),
                      reads=[pb_toks[bk]], writes=[mlx_toks[hh]])

        for tt in range(TOK // TT):
            t0 = tt * TT
            rms_to_hT(P, R, xT_d, gain_sb, gain_tok, hT, hT_toks, t0, pb[0], pb_toks[0])
            if tt > 0:
                for hh in range(8):
                    P.dve(lambda e, hh=hh: e.tensor_copy(out=mlx[:, hh, 0:3], in_=mlx[:, hh, TT:TT + 3]),
                          reads=[mlx_toks[hh]], writes=[mlx_toks[hh]])
            for j in range(NBLK_IN // 2):
                iw = load_w(j)
                for sub in range(2):
                    blk = 2 * j + sub
                    grp, hh = blk // 8, blk % 8
                    bk = nextbank()
                    for c in range(KC):
                        P.pe(lambda e, iw=iw, bk=bk, c=c, sub=sub: e.matmul(
                            pb[bk][:], lhsT=wbuf[iw][:, c, sub * 128:(sub + 1) * 128], rhs=hT[:, c, :],
                            start=(c == 0), stop=(c == KC - 1)), reads=[wtok[iw], hT_toks[c]], writes=[pb_toks[bk]])
                    if grp == 7:
                        P.act(lambda e, hh=hh, bk=bk: e.copy(out=mlx[:, hh, 3:3 + TT], in_=pb[bk][:]),
                              reads=[pb_toks[bk]], writes=[mlx_toks[hh]])
                        continue
                    ei = nextev()
                    if blk % 2 == 0:
                        P.act(lambda e, ei=ei, bk=bk: e.copy(out=ev[ei][:], in_=pb[bk][:]),
                              reads=[pb_toks[bk]], writes=[ev_toks[ei]])
                    else:
                        P.dve(lambda e, ei=ei, bk=bk: e.tensor_copy(out=ev[ei][:], in_=pb[bk][:]),
                              reads=[pb_toks[bk]], writes=[ev_toks[ei]])
                    P.dma("sp", cm_d[blk, :, t0:t0 + TT], ev[ei][:], reads=[ev_toks[ei]])
                    if grp == 8:
                        P.pool(lambda e, ei=ei, hh=hh: e.tensor_copy(out=catT[:, 16 + hh, :], in_=ev[ei][:]),
                               reads=[ev_toks[ei]], writes=[cat_toks[16 + hh]])
            for hh in range(8):
                a = hh % 2
                P.dve(lambda e, hh=hh, a=a: e.tensor_scalar(out=acc[a][:], in0=mlx[:, hh, 0:TT], scalar1=cw[:, hh, 0:1],
                                                            scalar2=cb[:, hh:hh + 1], op0=ALU.mult, op1=ALU.add),
                      reads=[mlx_toks[hh], const_tok], writes=[acc_toks[a]])
                for k in range(1, 4):
                    P.dve(lambda e, hh=hh, a=a, k=k: e.scalar_tensor_tensor(out=acc[a][:], in0=mlx[:, hh, k:k + TT],
                                                                            scalar=cw[:, hh, k:k + 1], in1=acc[a][:],
                                                                            op0=ALU.mult, op1=ALU.add),
                          reads=[mlx_toks[hh], acc_toks[a], const_tok], writes=[acc_toks[a]])
                ei = nextev()
                P.act(lambda e, ei=ei, a=a: e.activation(out=ev[ei][:], in_=acc[a][:], func=AF.Silu),
                      reads=[acc_toks[a]], writes=[ev_toks[ei]])
                P.dma("sp", cm_d[56 + hh, :, t0:t0 + TT], ev[ei][:], reads=[ev_toks[ei]])
                xcb_tok = acc_toks[a]
                P.pool(lambda e, ei=ei, hh=hh: e.tensor_copy(out=hT[:, hh, :], in_=ev[ei][:]),
                       reads=[ev_toks[ei]], writes=[hT_toks[hh]])
                for (wsb, base, coff) in ((wq, 80, 0), (wk, 88, 8)):
                    bk = nextbank()
                    P.pe(lambda e, wsb=wsb, hh=hh, bk=bk: e.matmul(pb[bk][:], lhsT=wsb[:, hh, :], rhs=hT[:, hh, :],
                                                                   start=True, stop=True),
                         reads=[const_tok, hT_toks[hh]], writes=[pb_toks[bk]])
                    ei2 = nextev()
                    P.act(lambda e, ei2=ei2, bk=bk: e.copy(out=ev[ei2][:], in_=pb[bk][:]),
                          reads=[pb_toks[bk]], writes=[ev_toks[ei2]])
                    P.dma("sp", cm_d[base + hh, :, t0:t0 + TT], ev[ei2][:], reads=[ev_toks[ei2]])
                    P.pool(lambda e, ei2=ei2, hh=hh, coff=coff: e.tensor_copy(out=catT[:, coff + hh, :], in_=ev[ei2][:]),
                           reads=[ev_toks[ei2]], writes=[cat_toks[coff + hh]])
            bk = nextbank()
            for c in range(24):
                P.pe(lambda e, c=c, bk=bk: e.matmul(pb[bk][0:16, :], lhsT=wif[:, c, :], rhs=catT[:, c, :],
                                                    start=(c == 0), stop=(c == 23)),
                     reads=[const_tok, cat_toks[c]], writes=[pb_toks[bk]])
            P.act(lambda e, bk=bk: e.activation(out=gsb[:], in_=pb[bk][0:16, :], func=AF.Identity, bias=bif[:, 0:1]),
                  reads=[pb_toks[bk], const_tok], writes=[gtok])
            P.dma("sp", g2_d[0, :, t0:t0 + TT], gsb[:], reads=[gtok])
            P.act(lambda e: e.activation(out=gsb2[:], in_=gsb[:], func=AF.Exp, scale=-1.0), reads=[gtok], writes=[gtok2])
            P.act(lambda e: e.activation(out=gsb2[:], in_=gsb2[:], func=AF.Ln, bias=1.0), reads=[gtok2], writes=[gtok2])
            P.dve(lambda e: e.tensor_scalar(out=gsb2[:], in0=gsb2[:], scalar1=-1.0, scalar2=None, op0=ALU.mult),
                  reads=[gtok2], writes=[gtok2])
            P.dma("sp", g2_d[1, :, t0:t0 + TT], gsb2[:], reads=[gtok2])
        P.finish()
    return nc


def run_p2(xT_list, l, inp):
    nc = get_prog("p2", build_p2)
    g = fm_vec(inp["mix_norm"][l])
    cw = np.ascontiguousarray(inp["ml_conv_w"][l].T.reshape(8, 128, 4).transpose(1, 0, 2))
    cb = np.ascontiguousarray(inp["ml_conv_b"][l].reshape(8, 128).T)
    wq = np.ascontiguousarray(inp["ml_w_q"][l].transpose(1, 0, 2))
    wk = np.ascontiguousarray(inp["ml_w_k"][l].transpose(1, 0, 2))
    wif = np.ascontiguousarray(inp["ml_w_if"][l].reshape(24, 128, 16).transpose(1, 0, 2))
    bif = np.ascontiguousarray(inp["ml_b_if"][l].reshape(16, 1))
    in_maps = []
    for i in range(NCORES):
        halo = np.zeros((128, KC, HALO), np.float32) if i == 0 else np.ascontiguousarray(xT_list[i - 1][:, :, TOK - HALO:])
        in_maps.append({"xT": xT_list[i], "xhalo": halo, "gain": g, "w_in": inp["w_in"][l], "conv_w": cw, "conv_b": cb,
                        "wq": wq, "wk": wk, "wif": wif, "bif": bif})
    res = run_bass_kernel_spmd(nc, in_maps, core_ids=list(range(NCORES)))
    cm = np.concatenate([r["cm"] for r in res.results], axis=2)
    g2 = np.concatenate([r["g2"] for r in res.results], axis=2)
    return cm, g2


NSB = S // 128
NTB = S // TT


def head_rms_cm(P, src, dst_fn, gain_ap, ones_bf, ones_tok, sq, sq_tok, ps, ps_tok, rstd, rstd_tok, src_tok, dst_tok, n, extra=1.0):
    P.act(lambda e: e.activation(out=sq[:, :n], in_=src, func=AF.Square), reads=[src_tok], writes=[sq_tok])
    P.pe(lambda e: e.matmul(ps[:, :n], lhsT=ones_bf[:], rhs=sq[:, :n], start=True, stop=True),
         reads=[sq_tok, ones_tok], writes=[ps_tok])
    P.dve(lambda e: e.tensor_scalar(out=rstd[:, :n], in0=ps[:, :n], scalar1=1.0 / HD, scalar2=EPS, op0=ALU.mult, op1=ALU.add),
          reads=[ps_tok], writes=[rstd_tok])
    P.act(lambda e: e.activation(out=rstd[:, :n], in_=rstd[:, :n], func=AF.Sqrt), reads=[rstd_tok], writes=[rstd_tok])
    P.dve(lambda e: e.reciprocal(out=rstd[:, :n], in_=rstd[:, :n]), reads=[rstd_tok], writes=[rstd_tok])
    if extra != 1.0:
        P.dve(lambda e: e.tensor_scalar(out=rstd[:, :n], in0=rstd[:, :n], scalar1=float(extra), scalar2=None, op0=ALU.mult),
              reads=[rstd_tok], writes=[rstd_tok])
    P.dve(lambda e: e.scalar_tensor_tensor(out=dst_fn, in0=src, scalar=gain_ap, in1=rstd[:, :n], op0=ALU.mult, op1=ALU.mult),
          reads=[src_tok, rstd_tok], writes=[dst_tok])


def build_pa():
    import contextlib
    nc = bass.Bass("TRN2", target_bir_lowering=False)
    qT_d = nc.dram_tensor("qT", [128, S], F32, kind="ExternalInput").ap()
    kT_d = nc.dram_tensor("kT", [128, S], F32, kind="ExternalInput").ap()
    v_d = nc.dram_tensor("v", [128, NSB, 128], F32, kind="ExternalInput").ap()
    gn_d = nc.dram_tensor("gains", [128, 2], F32, kind="ExternalInput").ap()
    mask_d = nc.dram_tensor("mask", [128, 4, TT], F32, kind="ExternalInput").ap()
    un_d = nc.dram_tensor("uneg", [128, 128], F32, kind="ExternalInput").ap()
    o_d = nc.dram_tensor("o", [128, NSB, 128], F32, kind="ExternalOutput").ap()
    P = Prog(nc)
    with contextlib.ExitStack() as st:
        sb = lambda name, shape, dt: st.enter_context(nc.sbuf_tensor("s_" + name, shape, dt))
        qn = sb("qn", [128, S], BF16)
        kn = sb("kn", [128, S], BF16)
        qn_toks, kn_toks = P.toks(NTB), P.toks(NTB)
        v = sb("v", [128, NSB, 128], BF16)
        v_tok = P.tok()
        gn = sb("gn", [128, 2], F32)
        mask = sb("mask", [128, 4, TT], BF16)
        uneg = sb("uneg", [128, 128], BF16)
        ones_bf = sb("ones", [128, 128], BF16)
        const_tok = P.tok()
        ld = [sb("ld%d" % i, [128, TT], F32) for i in range(2)]
        ld_toks = P.toks(2)
        sq = sb("sq", [128, TT], BF16)
        sq_tok = P.tok()
        rstd = sb("rstd", [128, TT], F32)
        rstd_tok = P.tok()
        E = [sb("E%d" % i, [128, TT], F32) for i in range(2)]
        SPt = [sb("SP%d" % i, [128, TT], BF16) for i in range(2)]
        AT = [sb("AT%d" % i, [128, TT], BF16) for i in range(2)]
        E_t, SP_t, AT_t = P.toks(2), P.toks(2), P.toks(2)
        oacc = [sb("oacc%d" % i, [128, 4, 128], F32) for i in range(2)]
        oacc_t = P.toks(2)
        csb = sb("csb", [128, 4], F32)
        wsb = sb("wsb", [128, 4], F32)
        csb_t, wsb_t = P.tok(), P.tok()
        Z = [st.enter_context(nc.psum_tensor("Z%d" % i, [128, TT], F32)) for i in range(2)]
        Z2 = [st.enter_context(nc.psum_tensor("Zb%d" % i, [128, TT], F32)) for i in range(2)]
        OB = [st.enter_context(nc.psum_tensor("OB%d" % i, [128, 4, 128], F32)) for i in range(2)]
        CB = st.enter_context(nc.psum_tensor("CB", [128, 4], F32))
        PS = st.enter_context(nc.psum_tensor("PS", [128, TT], F32))
        Z_t, Z2_t, OB_t = P.toks(2), P.toks(2), P.toks(2)
        CB_t, PS_t = P.tok(), P.tok()

        P.dma("sp", gn[:], gn_d[:, :], writes=[const_tok])
        P.dma("pool", mask[:], mask_d[:, :, :], writes=[const_tok])
        P.dma("pool", uneg[:], un_d[:, :], writes=[const_tok])
        P.dve(lambda e: e.memset(ones_bf[:], 1.0), writes=[const_tok])
        for q in range(4):
            P.dma("pool", v[:, q * 16:(q + 1) * 16, :], v_d[:, q * 16:(q + 1) * 16, :], writes=[v_tok])
        for i in range(NTB):
            for (src_d, dst, dtoks, gi, extra) in ((qT_d, qn, qn_toks, 0, HD ** -0.5), (kT_d, kn, kn_toks, 1, 1.0)):
                b = (2 * i + gi) % 2
                P.dma("sp", ld[b][:], src_d[:, i * TT:(i + 1) * TT], writes=[ld_toks[b]])
                head_rms_cm(P, ld[b][:], dst[:, i * TT:(i + 1) * TT], gn[:, gi:gi + 1], ones_bf, const_tok, sq, sq_tok,
                            PS, PS_t, rstd, rstd_tok, ld_toks[b], dtoks[i], TT, extra)
        stores = []
        pair = 0
        for Tb in range(NTB):
            oa, oa_t = oacc[Tb % 2], oacc_t[Tb % 2]
            P.pool(lambda e, oa=oa: e.memset(oa[:], 0.0), writes=[oa_t])
            P.dve(lambda e: e.memset(csb[:], 0.0), writes=[csb_t])
            qtile = qn[:, Tb * TT:(Tb + 1) * TT]
            for Sb in range(4 * Tb + 3, -1, -1):
                b = pair % 2
                pair += 1
                r = Sb - 4 * Tb
                kblk = kn[:, Sb * 128:(Sb + 1) * 128]
                rd = [kn_toks[Sb // 4], qn_toks[Tb]]
                P.pe(lambda e, b=b, kblk=kblk, qtile=qtile: e.matmul(Z[b][:], lhsT=kblk, rhs=qtile, start=True, stop=True),
                     reads=rd, writes=[Z_t[b]])
                P.act(lambda e, b=b: e.activation(out=E[b][:], in_=Z[b][:], func=AF.Exp), reads=[Z_t[b]], writes=[E_t[b]])
                P.act(lambda e, b=b: e.activation(out=SPt[b][:], in_=E[b][:], func=AF.Ln, bias=1.0), reads=[E_t[b]], writes=[SP_t[b]])
                if r >= 0:
                    P.pool(lambda e, b=b, r=r: e.tensor_tensor(out=SPt[b][:], in0=SPt[b][:], in1=mask[:, r, :], op=ALU.mult),
                           reads=[SP_t[b], const_tok], writes=[SP_t[b]])
                P.pe(lambda e, b=b, kblk=kblk, qtile=qtile: e.matmul(Z2[b][:], lhsT=kblk, rhs=qtile, start=True, stop=False),
                     reads=rd, writes=[Z2_t[b]])
                P.pe(lambda e, b=b: e.matmul(Z2[b][:], lhsT=uneg[:], rhs=SPt[b][:], start=False, stop=True),
                     reads=[SP_t[b], const_tok], writes=[Z2_t[b]])
                P.act(lambda e, b=b: e.activation(out=AT[b][:], in_=Z2[b][:], func=AF.Exp), reads=[Z2_t[b]], writes=[AT_t[b]])
                if r >= 0:
                    P.pool(lambda e, b=b, r=r: e.tensor_tensor(out=AT[b][:], in0=AT[b][:], in1=mask[:, r, :], op=ALU.mult),
                           reads=[AT_t[b], const_tok], writes=[AT_t[b]])
                for g in range(4):
                    P.pe(lambda e, b=b, g=g, Sb=Sb: e.matmul(OB[b][:, g, :], lhsT=AT[b][:, g * 128:(g + 1) * 128], rhs=v[:, Sb, :],
                                                           start=True, stop=True),
                         reads=[AT_t[b], v_tok], writes=[OB_t[b]])
                for g in range(4):
                    P.pe(lambda e, b=b, g=g: e.matmul(CB[:, g:g + 1], lhsT=SPt[b][:, g * 128:(g + 1) * 128], rhs=ones_bf[:, 0:1],
                                                      start=True, stop=True),
                         reads=[SP_t[b], const_tok], writes=[CB_t])
                P.act(lambda e: e.activation(out=wsb[:], in_=csb[:], func=AF.Exp, scale=-1.0), reads=[csb_t], writes=[wsb_t])
                for g in range(4):
                    P.dve(lambda e, b=b, g=g, oa=oa: e.scalar_tensor_tensor(out=oa[:, g, :], in0=OB[b][:, g, :], scalar=wsb[:, g:g + 1],
                                                                          in1=oa[:, g, :], op0=ALU.mult, op1=ALU.add),
                          reads=[OB_t[b], wsb_t, oa_t], writes=[oa_t])
                P.dve(lambda e: e.tensor_tensor(out=csb[:], in0=csb[:], in1=CB[:, :], op=ALU.add), reads=[csb_t, CB_t], writes=[csb_t])
            stores.append(P.dma("sp", o_d[:, Tb * 4:(Tb + 1) * 4, :], oa[:], reads=[oa_t]))
        P.finish()
    return nc


def attn_consts():
    s = np.arange(128)[:, None, None]
    r = np.arange(4)[None, :, None]
    t = np.arange(TT)[None, None, :]
    mask = ((r * 128 + s) < t).astype(np.float32)
    j = np.arange(128)[:, None]
    ss = np.arange(128)[None, :]
    uneg = -(j >= ss).astype(np.float32)
    return np.ascontiguousarray(mask), np.ascontiguousarray(uneg)


def run_pa(cm, l, inp):
    nc = get_prog("pa", build_pa)
    mask, uneg = attn_consts()
    gains = np.ascontiguousarray(np.stack([inp["sb_q_gain"][l], inp["sb_k_gain"][l]], axis=1))
    in_maps = []
    for h in range(NCORES):
        vtm = np.ascontiguousarray(cm[16 + h].T.reshape(NSB, 128, 128).transpose(1, 0, 2))
        in_maps.append({"qT": np.ascontiguousarray(cm[h]), "kT": np.ascontiguousarray(cm[8 + h]), "v": vtm,
                        "gains": gains, "mask": mask, "uneg": uneg})
    res = run_bass_kernel_spmd(nc, in_maps, core_ids=list(range(NCORES)))
    return [np.ascontiguousarray(r["o"].transpose(1, 0, 2).reshape(S, 128)) for r in res.results]


CH = 64
NCH = S // CH
CPT = TT // CH


def build_pb():
    import contextlib
    nc = bass.Bass("TRN2", target_bir_lowering=False)
    fT_d = nc.dram_tensor("fT", [128, S], F32, kind="ExternalInput").ap()
    qT_d = nc.dram_tensor("qT", [128, S], F32, kind="ExternalInput").ap()
    v_d = nc.dram_tensor("v", [CH, NCH, 128], F32, kind="ExternalInput").ap()
    lbl_d = nc.dram_tensor("lbl", [128, 4], F32, kind="ExternalInput").ap()
    sel_d = nc.dram_tensor("sel", [128, 4], F32, kind="ExternalInput").ap()
    mask_d = nc.dram_tensor("mask", [CH, CPT, CH], F32, kind="ExternalInput").ap()
    id_d = nc.dram_tensor("ident", [128, 128], F32, kind="ExternalInput").ap()
    o_d = nc.dram_tensor("o", [CH, NCH, 128], F32, kind="ExternalOutput").ap()
    P = Prog(nc)
    with contextlib.ExitStack() as st:
        sb = lambda name, shape, dt: st.enter_context(nc.sbuf_tensor("s_" + name, shape, dt))
        v = sb("v", [CH, NCH, 128], BF16)
        v_tok = P.tok()
        lbl = sb("lbl", [128, 4], F32)
        sel = sb("sel", [128, 4], F32)
        lbt = sb("lbt", [128, 4], F32)
        lb = sb("lb", [128, 1], F32)
        oml = sb("oml", [128, 1], F32)
        ssum = sb("ssum", [128, 1], F32)
        mask = sb("mask", [CH, CPT, CH], F32)
        ident = sb("ident", [128, 128], BF16)
        const_tok = P.tok()
        fl = sb("fl", [128, TT], F32)
        ql = sb("ql", [128, TT], F32)
        fl_t, ql_t = P.tok(), P.tok()
        kk = sb("kk", [128, TT], F32)
        kk_t = P.tok()
        cumA = sb("cumA", [128, TT], F32)
        cumB = sb("cumB", [128, TT], F32)
        cumA_t, cumB_t = P.tok(), P.tok()
        dq = sb("dq", [128, TT], F32)
        dl = sb("dl", [128, TT], F32)
        dq_t, dl_t = P.tok(), P.tok()
        ex = [sb("ex%d" % i, [128, TT], F32) for i in range(4)]
        ex_t = P.toks(4)
        qm = sb("qm", [128, TT], BF16)
        km = sb("km", [128, TT], BF16)
        qd = sb("qd", [128, TT], BF16)
        kd = sb("kd", [128, TT], BF16)
        qm_t, km_t, qd_t, kd_t = P.toks(4)
        elast = sb("elast", [128, CPT], F32)
        elast_t = P.tok()
        attT = sb("attT", [CH, CPT, CH], BF16)
        attT_t = P.tok()
        kdtm = sb("kdtm", [CH, CPT, 128], BF16)
        kdtm_t = P.tok()
        Sf = sb("Sf", [128, 128], F32)
        Sb_ = sb("Sb", [128, 128], BF16)
        Sf_t, Sb_t = P.tok(), P.tok()
        osb = [sb("osb%d" % i, [CH, CPT, 128], F32) for i in range(2)]
        osb_t = P.toks(2)
        PA = st.enter_context(nc.psum_tensor("PA", [CH, CPT, CH], F32))
        PT = st.enter_context(nc.psum_tensor("PT", [CH, CPT, 128], BF16))
        PO = [st.enter_context(nc.psum_tensor("PO%d" % i, [CH, 4, 128], F32)) for i in range(2)]
        PU = [st.enter_context(nc.psum_tensor("PU%d" % i, [128, 128], F32)) for i in range(2)]
        PA_t, PT_t = P.tok(), P.tok()
        PO_t, PU_t = P.toks(2), P.toks(2)

        P.dma("sp", lbl[:], lbl_d[:, :], writes=[const_tok])
        P.dma("sp", sel[:], sel_d[:, :], writes=[const_tok])
        P.dma("sp", mask[:], mask_d[:, :, :], writes=[const_tok])
        P.dma("pool", ident[:], id_d[:, :], writes=[const_tok])
        for q in range(4):
            P.dma("pool", v[:, q * 32:(q + 1) * 32, :], v_d[:, q * 32:(q + 1) * 32, :], writes=[v_tok])
        lb_t = P.tok()
        P.act(lambda e: e.activation(out=lbt[:], in_=lbl[:], func=AF.Exp), reads=[const_tok], writes=[lb_t])
        P.dve(lambda e: e.reduce_sum(out=ssum[:], in_=lbt[:], axis=AX.X), reads=[lb_t], writes=[lb_t])
        P.dve(lambda e: e.reciprocal(out=ssum[:], in_=ssum[:]), reads=[lb_t], writes=[lb_t])
        P.dve(lambda e: e.tensor_tensor(out=lbt[:], in0=lbt[:], in1=sel[:], op=ALU.mult), reads=[lb_t, const_tok], writes=[lb_t])
        P.dve(lambda e: e.reduce_sum(out=lb[:], in_=lbt[:], axis=AX.X), reads=[lb_t], writes=[lb_t])
        P.dve(lambda e: e.tensor_tensor(out=lb[:], in0=lb[:], in1=ssum[:], op=ALU.mult), reads=[lb_t], writes=[lb_t])
        P.dve(lambda e: e.tensor_scalar(out=oml[:], in0=lb[:], scalar1=-1.0, scalar2=1.0, op0=ALU.mult, op1=ALU.add),
              reads=[lb_t], writes=[lb_t])
        P.dve(lambda e: e.memset(Sf[:], 0.0), writes=[Sf_t])
        P.dve(lambda e: e.memset(Sb_[:], 0.0), writes=[Sb_t])

        v3 = lambda ap: ap.rearrange("p (c l) -> p c l", l=CH)
        for i in range(NTB):
            P.dma("sp", fl[:], fT_d[:, i * TT:(i + 1) * TT], writes=[fl_t])
            P.dma("sp", ql[:], qT_d[:, i * TT:(i + 1) * TT], writes=[ql_t])
            P.act(lambda e: e.activation(out=fl[:], in_=fl[:], func=AF.Sigmoid), reads=[fl_t], writes=[fl_t])
            P.dve(lambda e: e.tensor_scalar(out=fl[:], in0=fl[:], scalar1=oml[:, 0:1], scalar2=lb[:, 0:1], op0=ALU.mult, op1=ALU.add),
                  reads=[fl_t, lb_t], writes=[fl_t])
            P.dve(lambda e: e.tensor_scalar(out=kk[:], in0=fl[:], scalar1=-1.0, scalar2=1.0, op0=ALU.mult, op1=ALU.add),
                  reads=[fl_t], writes=[kk_t])
            P.act(lambda e: e.activation(out=cumA[:], in_=fl[:], func=AF.Ln), reads=[fl_t], writes=[cumA_t])
            P.act(lambda e: e.activation(out=ql[:], in_=ql[:], func=AF.Silu), reads=[ql_t], writes=[ql_t])
            src, dst, src_t, dst_t = cumA, cumB, cumA_t, cumB_t
            for k in (1, 2, 4, 8, 16, 32):
                P.dve(lambda e, src=src, dst=dst, k=k: e.tensor_copy(out=v3(dst[:])[:, :, 0:k], in_=v3(src[:])[:, :, 0:k]),
                      reads=[src_t], writes=[dst_t])
                P.dve(lambda e, src=src, dst=dst, k=k: e.tensor_tensor(out=v3(dst[:])[:, :, k:CH], in0=v3(src[:])[:, :, k:CH],
                                                                       in1=v3(src[:])[:, :, 0:CH - k], op=ALU.add),
                      reads=[src_t], writes=[dst_t])
                src, dst, src_t, dst_t = dst, src, dst_t, src_t
            cum, cum_t = src, src_t
            c3 = v3(cum[:])
            P.dve(lambda e, c3=c3: e.tensor_tensor(out=v3(dq[:]), in0=c3, in1=c3[:, :, CH // 2 - 1:CH // 2].to_broadcast([128, CPT, CH]),
                                                   op=ALU.subtract), reads=[cum_t], writes=[dq_t])
            P.dve(lambda e, c3=c3: e.tensor_tensor(out=v3(dl[:]), in0=c3[:, :, CH - 1:CH].to_broadcast([128, CPT, CH]), in1=c3,
                                                   op=ALU.subtract), reads=[cum_t], writes=[dl_t])
            P.act(lambda e: e.activation(out=ex[0][:], in_=dq[:], func=AF.Exp), reads=[dq_t], writes=[ex_t[0]])
            P.act(lambda e: e.activation(out=ex[1][:], in_=dq[:], func=AF.Exp, scale=-1.0), reads=[dq_t], writes=[ex_t[1]])
            P.act(lambda e, cum=cum: e.activation(out=ex[2][:], in_=cum[:], func=AF.Exp), reads=[cum_t], writes=[ex_t[2]])
            P.act(lambda e: e.activation(out=ex[3][:], in_=dl[:], func=AF.Exp), reads=[dl_t], writes=[ex_t[3]])
            P.act(lambda e, c3=c3: e.activation(out=elast[:], in_=c3[:, :, CH - 1], func=AF.Exp), reads=[cum_t], writes=[elast_t])
            P.dve(lambda e: e.tensor_tensor(out=qm[:], in0=ql[:], in1=ex[0][:], op=ALU.mult), reads=[ql_t, ex_t[0]], writes=[qm_t])
            P.pool(lambda e: e.tensor_tensor(out=km[:], in0=kk[:], in1=ex[1][:], op=ALU.mult), reads=[kk_t, ex_t[1]], writes=[km_t])
            P.dve(lambda e: e.tensor_tensor(out=qd[:], in0=ql[:], in1=ex[2][:], op=ALU.mult), reads=[ql_t, ex_t[2]], writes=[qd_t])
            P.pool(lambda e: e.tensor_tensor(out=kd[:], in0=kk[:], in1=ex[3][:], op=ALU.mult), reads=[kk_t, ex_t[3]], writes=[kd_t])
            for c in range(CPT):
                P.pe(lambda e, c=c: e.matmul(PA[:, c, :], lhsT=km[:, c * CH:(c + 1) * CH], rhs=qm[:, c * CH:(c + 1) * CH], start=True, stop=True),
                     reads=[km_t, qm_t], writes=[PA_t])
            P.dve(lambda e: e.tensor_tensor(out=attT[:], in0=PA[:], in1=mask[:], op=ALU.mult), reads=[PA_t, const_tok], writes=[attT_t])
            for c in range(CPT):
                P.pe(lambda e, c=c: e.transpose(PT[:, c, :], kd[:, c * CH:(c + 1) * CH], ident[:]),
                     reads=[kd_t, const_tok], writes=[PT_t])
            P.act(lambda e: e.copy(out=kdtm[:], in_=PT[:]), reads=[PT_t], writes=[kdtm_t])
            ob, ob_t = osb[i % 2], osb_t[i % 2]
            for c in range(CPT):
                cg = i * CPT + c
                po, po_t = PO[(c // 4) % 2], PO_t[(c // 4) % 2]
                pu, pu_t = PU[c % 2], PU_t[c % 2]
                P.pe(lambda e, c=c, cg=cg, po=po: e.matmul(po[:, c % 4, :], lhsT=attT[:, c, :], rhs=v[:, cg, :], start=True, stop=False),
                     reads=[attT_t, v_tok], writes=[po_t])
                P.pe(lambda e, c=c, po=po: e.matmul(po[:, c % 4, :], lhsT=qd[:, c * CH:(c + 1) * CH], rhs=Sb_[:], start=False, stop=True),
                     reads=[qd_t, Sb_t], writes=[po_t])
                P.pe(lambda e, c=c, cg=cg, pu=pu: e.matmul(pu[:], lhsT=kdtm[:, c, :], rhs=v[:, cg, :], start=True, stop=True),
                     reads=[kdtm_t, v_tok], writes=[pu_t])
                P.dve(lambda e, c=c, pu=pu: e.scalar_tensor_tensor(out=Sf[:], in0=Sf[:], scalar=elast[:, c:c + 1], in1=pu[:],
                                                                 op0=ALU.mult, op1=ALU.add),
                      reads=[Sf_t, elast_t, pu_t], writes=[Sf_t])
                P.act(lambda e: e.copy(out=Sb_[:], in_=Sf[:]), reads=[Sf_t], writes=[Sb_t])
                if c % 4 == 3:
                    P.dve(lambda e, c=c, po=po, ob=ob: e.tensor_copy(out=ob[:, c - 3:c + 1, :], in_=po[:]), reads=[po_t], writes=[ob_t])
            P.dma("sp", o_d[:, i * CPT:(i + 1) * CPT, :], ob[:], reads=[ob_t])
        P.finish()
    return nc


def run_pb(cm, l, inp):
    nc = get_prog("pb", build_pb)
    s_ = np.arange(CH)[:, None, None]
    t_ = np.arange(CH)[None, None, :]
    mask = np.ascontiguousarray(np.broadcast_to((s_ <= t_).astype(np.float32), (CH, CPT, CH)))
    ident = np.eye(128, dtype=np.float32)
    sel = np.zeros((128, 4), np.float32)
    sel[:, 1:l + 1] = 1.0
    in_maps = []
    for h in range(NCORES):
        vtm = np.ascontiguousarray(cm[32 + h].T.reshape(NCH, CH, 128).transpose(1, 0, 2))
        lbl = np.ascontiguousarray(inp["hg_lb_logits"][:, h * 128:(h + 1) * 128].T)
        in_maps.append({"fT": np.ascontiguousarray(cm[24 + h]), "qT": np.ascontiguousarray(cm[40 + h]), "v": vtm,
                        "lbl": lbl, "sel": sel, "mask": mask, "ident": ident})
    res = run_bass_kernel_spmd(nc, in_maps, core_ids=list(range(NCORES)))
    return [np.ascontiguousarray(r["o"].transpose(1, 0, 2).reshape(S, 128)) for r in res.results]


LC = 128
NLC = S // LC
NEG = -1.0e30


def build_pc():
    import contextlib
    nc = bass.Bass("TRN2", target_bir_lowering=False)
    qT_d = nc.dram_tensor("qT", [128, S], F32, kind="ExternalInput").ap()
    kT_d = nc.dram_tensor("kT", [128, S], F32, kind="ExternalInput").ap()
    ktm_d = nc.dram_tensor("ktm", [128, NLC, 128], F32, kind="ExternalInput").ap()
    v_d = nc.dram_tensor("v", [128, NLC, 128], F32, kind="ExternalInput").ap()
    li_d = nc.dram_tensor("li", [128, NLC], F32, kind="ExternalInput").ap()
    lf_d = nc.dram_tensor("lf", [128, NLC], F32, kind="ExternalInput").ap()
    tri_d = nc.dram_tensor("tri", [128, 128], F32, kind="ExternalInput").ap()
    nm_d = nc.dram_tensor("negmask", [128, 128], F32, kind="ExternalInput").ap()
    o_d = nc.dram_tensor("o", [128, NLC, 128], F32, kind="ExternalOutput").ap()
    P = Prog(nc)
    with contextlib.ExitStack() as st:
        sb = lambda name, shape, dt: st.enter_context(nc.sbuf_tensor("s_" + name, shape, dt))
        qT = sb("qT", [128, S], BF16)
        kT = sb("kT", [128, S], BF16)
        ktm = sb("ktm", [128, NLC, 128], F32)
        vx = sb("vx", [128, NLC, 129], BF16)
        li = sb("li", [128, NLC], F32)
        lf = sb("lf", [128, NLC], F32)
        tri = sb("tri", [128, 128], F32)
        negmask = sb("negmask", [128, 128], F32)
        ones_f = sb("ones_f", [128, 128], F32)
        in_tok = P.tok()
        stage = [sb("stg%d" % i, [128, TT], F32) for i in range(2)]
        stage_t = P.toks(2)
        qk_t = P.toks(2 * NTB)
        a_all = sb("a_all", [128, NLC], F32)
        b_all = sb("b_all", [128, NLC], F32)
        ab_t = P.tok()
        lfB = [sb("lfB%d" % i, [128, 128], F32) for i in range(2)]
        lfB_t = P.toks(2)
        tmp = [sb("tmp%d" % i, [128, 128], F32) for i in range(2)]
        tmp_t = P.toks(2)
        Dm = [sb("Dm%d" % i, [128, 128], F32) for i in range(2)]
        Dm_t = P.toks(2)
        Ae = [sb("Ae%d" % i, [128, 128], F32) for i in range(2)]
        Ae_t = P.toks(2)
        WT = [sb("WT%d" % i, [128, 128], BF16) for i in range(2)]
        WT_t = P.toks(2)
        qa = [sb("qa%d" % i, [128, 128], BF16) for i in range(2)]
        qa_t = P.toks(2)
        wcol = [sb("wcol%d" % i, [128, 2], F32) for i in range(2)]
        wcol_t = P.toks(2)
        kw = [sb("kw%d" % i, [128, 128], BF16) for i in range(2)]
        kw_t = P.toks(2)
        Cf = sb("Cf", [128, 129], F32)
        Cb = sb("Cb", [128, 129], BF16)
        Cf_t, Cb_t = P.tok(), P.tok()
        dn = [sb("dn%d" % i, [128, 1], F32) for i in range(2)]
        dn_t = P.toks(2)
        osb = [sb("osb%d" % i, [128, 8, 128], F32) for i in range(2)]
        osb_t = P.toks(2)
        PAB = [st.enter_context(nc.psum_tensor("PAB%d" % i, [128, 128], F32)) for i in range(2)]
        PQK = [st.enter_context(nc.psum_tensor("PQK%d" % i, [128, 128], F32)) for i in range(2)]
        PN = [st.enter_context(nc.psum_tensor("PN%d" % i, [128, 129], F32)) for i in range(2)]
        PU = st.enter_context(nc.psum_tensor("PU", [128, 129], F32))
        PC = st.enter_context(nc.psum_tensor("PC", [128, NLC], F32))
        PAB_t, PQK_t, PN_t = P.toks(2), P.toks(2), P.toks(2)
        PU_t, PC_t = P.tok(), P.tok()

        P.dma("sp", li[:], li_d[:, :], writes=[in_tok])
        P.dma("sp", lf[:], lf_d[:, :], writes=[in_tok])
        P.dma("sp", tri[:], tri_d[:, :], writes=[in_tok])
        P.dma("sp", negmask[:], nm_d[:, :], writes=[in_tok])
        for q in range(4):
            P.dma("sp", ktm[:, q * 16:(q + 1) * 16, :], ktm_d[:, q * 16:(q + 1) * 16, :], writes=[in_tok])
            P.dma("pool", vx[:, q * 16:(q + 1) * 16, 0:128], v_d[:, q * 16:(q + 1) * 16, :], writes=[in_tok])
        P.dve(lambda e: e.memset(vx[:, :, 128:129], 1.0), writes=[in_tok])
        P.dve(lambda e: e.memset(ones_f[:], 1.0), writes=[in_tok])
        P.dve(lambda e: e.memset(Cf[:], 0.0), writes=[Cf_t])
        P.dve(lambda e: e.memset(Cb[:], 0.0), writes=[Cb_t])
        SC = float(HD) ** -0.5
        for i in range(NTB):
            for (src_d, dst, off, sc) in ((qT_d, qT, 0, 1.0), (kT_d, kT, 1, SC)):
                b = (2 * i + off) % 2
                P.dma("sp", stage[b][:], src_d[:, i * TT:(i + 1) * TT], writes=[stage_t[b]])
                P.act(lambda e, b=b, dst=dst, i=i, sc=sc: e.activation(out=dst[:, i * TT:(i + 1) * TT], in_=stage[b][:], func=AF.Copy, scale=sc),
                      reads=[stage_t[b]], writes=[qk_t[2 * i + off]])
        P.pe(lambda e: e.matmul(PC[:], lhsT=tri[:], rhs=lf[:], start=True, stop=True), reads=[in_tok], writes=[PC_t])
        P.dve(lambda e: e.tensor_copy(out=a_all[:], in_=PC[:]), reads=[PC_t], writes=[ab_t])
        P.dve(lambda e: e.tensor_tensor(out=b_all[:], in0=li[:], in1=a_all[:], op=ALU.subtract), reads=[in_tok, ab_t], writes=[ab_t])
        for c in range(NLC):
            b = c % 2
            ti = c // 4
            csl = slice(c * LC, (c + 1) * LC)
            P.dve(lambda e, b=b, c=c: e.tensor_scalar(out=lfB[b][:], in0=ones_f[:], scalar1=lf[:, c:c + 1], scalar2=None, op0=ALU.mult),
                  reads=[in_tok], writes=[lfB_t[b]])
            P.pe(lambda e, b=b: e.matmul(PAB[b][:], lhsT=lfB[b][:], rhs=tri[:], start=True, stop=True),
                 reads=[lfB_t[b], in_tok], writes=[PAB_t[b]])
            P.pe(lambda e, b=b, csl=csl: e.matmul(PQK[b][:], lhsT=kT[:, csl], rhs=qT[:, csl], start=True, stop=True),
                 reads=[qk_t[2 * ti], qk_t[2 * ti + 1]], writes=[PQK_t[b]])
            P.dve(lambda e, b=b: e.tensor_tensor(out=tmp[b][:], in0=PAB[b][:], in1=negmask[:], op=ALU.add),
                  reads=[PAB_t[b], in_tok], writes=[tmp_t[b]])
            P.act(lambda e, b=b, c=c: e.activation(out=Dm[b][:], in_=tmp[b][:], func=AF.Exp, bias=b_all[:, c:c + 1]),
                  reads=[tmp_t[b], ab_t], writes=[Dm_t[b]])
            P.act(lambda e, b=b: e.activation(out=Ae[b][:], in_=PAB[b][:], func=AF.Exp), reads=[PAB_t[b]], writes=[Ae_t[b]])
            P.act(lambda e, b=b, c=c: e.activation(out=wcol[b][:, 0:1], in_=PAB[b][:, LC - 1:LC], func=AF.Exp, bias=b_all[:, c:c + 1]),
                  reads=[PAB_t[b], ab_t], writes=[wcol_t[b]])
            P.act(lambda e, b=b: e.activation(out=wcol[b][:, 1:2], in_=PAB[b][:, LC - 1:LC], func=AF.Exp),
                  reads=[PAB_t[b]], writes=[wcol_t[b]])
            P.dve(lambda e, b=b: e.tensor_tensor(out=WT[b][:], in0=PQK[b][:], in1=Dm[b][:], op=ALU.mult),
                  reads=[PQK_t[b], Dm_t[b]], writes=[WT_t[b]])
            P.pool(lambda e, b=b, csl=csl: e.tensor_tensor(out=qa[b][:], in0=qT[:, csl], in1=Ae[b][:], op=ALU.mult),
                   reads=[qk_t[2 * ti], Ae_t[b]], writes=[qa_t[b]])
            P.pe(lambda e, b=b, c=c: e.matmul(PN[b][:], lhsT=WT[b][:], rhs=vx[:, c, :], start=True, stop=False),
                 reads=[WT_t[b], in_tok], writes=[PN_t[b]])
            P.pe(lambda e, b=b: e.matmul(PN[b][:], lhsT=qa[b][:], rhs=Cb[:], start=False, stop=True),
                 reads=[qa_t[b], Cb_t], writes=[PN_t[b]])
            P.act(lambda e, b=b: e.activation(out=dn[b][:], in_=PN[b][:, 128:129], func=AF.Abs), reads=[PN_t[b]], writes=[dn_t[b]])
            P.dve(lambda e, b=b: e.tensor_scalar(out=dn[b][:], in0=dn[b][:], scalar1=1.0, scalar2=None, op0=ALU.max),
                  reads=[dn_t[b]], writes=[dn_t[b]])
            P.dve(lambda e, b=b: e.reciprocal(out=dn[b][:], in_=dn[b][:]), reads=[dn_t[b]], writes=[dn_t[b]])
            ob, ob_t = osb[(c // 8) % 2], osb_t[(c // 8) % 2]
            P.dve(lambda e, b=b, c=c, ob=ob: e.tensor_scalar(out=ob[:, c % 8, :], in0=PN[b][:, 0:128], scalar1=dn[b][:, 0:1], scalar2=None, op0=ALU.mult),
                  reads=[PN_t[b], dn_t[b]], writes=[ob_t])
            P.pool(lambda e, b=b, c=c: e.tensor_scalar(out=kw[b][:], in0=ktm[:, c, :], scalar1=wcol[b][:, 0:1], scalar2=SC, op0=ALU.mult, op1=ALU.mult),
                   reads=[in_tok, wcol_t[b]], writes=[kw_t[b]])
            P.pe(lambda e, b=b, c=c: e.matmul(PU[:], lhsT=kw[b][:], rhs=vx[:, c, :], start=True, stop=True),
                 reads=[kw_t[b], in_tok], writes=[PU_t])
            P.dve(lambda e, b=b: e.scalar_tensor_tensor(out=Cf[:], in0=Cf[:], scalar=wcol[b][:, 1:2], in1=PU[:], op0=ALU.mult, op1=ALU.add),
                  reads=[Cf_t, wcol_t[b], PU_t], writes=[Cf_t])
            P.act(lambda e: e.copy(out=Cb[:], in_=Cf[:]), reads=[Cf_t], writes=[Cb_t])
            if c % 8 == 7:
                P.dma("sp", o_d[:, c - 7:c + 1, :], ob[:], reads=[ob_t])
        P.finish()
    return nc


def run_pc(cm, g2, l, inp):
    nc = get_prog("pc", build_pc)
    j = np.arange(128)[:, None]
    t = np.arange(128)[None, :]
    tri = (j <= t).astype(np.float32)
    negmask = np.where(j <= t, 0.0, NEG).astype(np.float32)
    tm = lambda a: np.ascontiguousarray(a.T.reshape(NLC, LC, 128).transpose(1, 0, 2))
    col = lambda a: np.ascontiguousarray(a.reshape(NLC, LC).T)
    in_maps = []
    for h in range(NCORES):
        in_maps.append({"qT": np.ascontiguousarray(cm[80 + h]), "kT": np.ascontiguousarray(cm[88 + h]), "ktm": tm(cm[88 + h]),
                        "v": tm(cm[64 + h]), "li": col(g2[0, h]), "lf": col(g2[1, 8 + h]), "tri": tri, "negmask": negmask})
    res = run_bass_kernel_spmd(nc, in_maps, core_ids=list(range(NCORES)))
    return [np.ascontiguousarray(r["o"].transpose(1, 0, 2).reshape(S, 128)) for r in res.results]


def build_p4():
    import contextlib
    nc = bass.Bass("TRN2", target_bir_lowering=False)
    xT_d = nc.dram_tensor("xT", [128, KC, TOK], F32, kind="ExternalInput").ap()
    gain_d = nc.dram_tensor("gain", [128, KC], F32, kind="ExternalInput").ap()
    br_d = nc.dram_tensor("br", [6, 8, 128, TOK], F32, kind="ExternalInput").ap()
    sg_d = nc.dram_tensor("smallg", [128, 10], F32, kind="ExternalInput").ap()
    wmg_d = nc.dram_tensor("wmg", [3, D, D], F32, kind="ExternalInput").ap()
    wbr_d = nc.dram_tensor("wbr", [3, W_MIX, D], F32, kind="ExternalInput").ap()
    wout_d = nc.dram_tensor("wout", [D, D], F32, kind="ExternalInput").ap()
    oT_d = nc.dram_tensor("oT", [128, KC, TOK], F32, kind="ExternalOutput").ap()
    scr_d = nc.dram_tensor("mscr", [128, KC, TOK], BF16, kind="ExternalOutput").ap()
    P = Prog(nc)
    NT = TOK // TT
    with contextlib.ExitStack() as st:
        sb = lambda name, shape, dt: st.enter_context(nc.sbuf_tensor("s_" + name, shape, dt))
        hT = sb("hT", [128, KC, TOK], BF16)
        hT_toks = [P.toks(KC) for _ in range(NT)]
        yT = sb("yT", [128, 24, TOK], BF16)
        yT_toks = [P.toks(24) for _ in range(NT)]
        mst = [sb("mst%d" % i, [128, TT], BF16) for i in range(2)]
        mst_t = P.toks(2)
        scr_tok = P.tok()
        NWB = 2
        wbuf = [sb("w%d" % i, [128, KC, WCOLS], BF16) for i in range(NWB)]
        wtok = P.toks(NWB)
        NBB = 2
        bbuf = [sb("bw%d" % i, [128, 8, WCOLS], BF16) for i in range(NBB)]
        btok = P.toks(NBB)
        gain_sb = sb("gain_sb", [128, KC], F32)
        gain_tok = P.tok()
        sg = sb("sg", [128, 10], F32)
        sg_tok = P.tok()
        R = rms_resources(P, sb)
        ld = [sb("ld%d" % i, [128, TT], F32) for i in range(3)]
        ld_t = P.toks(3)
        nrm = sb("nrm", [128, TT], F32)
        nrm_t = P.tok()
        sl = sb("slu", [128, TT], F32)
        sl_t = P.tok()
        sq = sb("hsq", [128, TT], BF16)
        sq_t = P.tok()
        hr = sb("hrstd", [128, TT], F32)
        hr_t = P.tok()
        sgm = [sb("sgm%d" % i, [128, TT], F32) for i in range(2)]
        sgm_t = P.toks(2)
        prod = [sb("prod%d" % i, [128, TT], F32) for i in range(2)]
        prod_t = P.toks(2)
        macc = [sb("macc%d" % i, [128, TT], F32) for i in range(4)]
        macc_t = P.toks(4)
        ot = [sb("ot%d" % i, [128, TT], F32) for i in range(2)]
        ot_toks = P.toks(2)
        xr = [sb("xr%d" % i, [128, TT], F32) for i in range(2)]
        xr_toks = P.toks(2)
        pb = [st.enter_context(nc.psum_tensor("pb%d" % i, [128, TT], F32)) for i in range(8)]
        pb_toks = P.toks(8)
        P.dma("sp", gain_sb[:], gain_d[:, :], writes=[gain_tok])
        P.dma("sp", sg[:], sg_d[:, :], writes=[sg_tok])
        wcount, bcount, pcount, lcount = [0], [0], [0], [0]

        def load_w(w2d, j):
            i = wcount[0] % NWB
            wcount[0] += 1
            src = w2d[:, j * WCOLS:(j + 1) * WCOLS].rearrange("(c p) n -> p c n", p=128)
            for q in range(4):
                P.dma("pool", wbuf[i][:, q * 8:(q + 1) * 8, :], src[:, q * 8:(q + 1) * 8, :], writes=[wtok[i]])
            return i

        def load_b(w2d, j):
            i = bcount[0] % NBB
            bcount[0] += 1
            src = w2d[:, j * WCOLS:(j + 1) * WCOLS].rearrange("(c p) n -> p c n", p=128)
            P.dma("pool", bbuf[i][:], src, writes=[btok[i]])
            return i

        def nextbank():
            i = pcount[0] % 8
            pcount[0] += 1
            return i

        def load_tile(kind, h, t0):
            i = lcount[0] % 3
            lcount[0] += 1
            P.dma("sp", ld[i][:], br_d[kind, h, :, t0:t0 + TT], writes=[ld_t[i]])
            return i

        for tt in range(NT):
            t0 = tt * TT
            tsl = slice(t0, t0 + TT)
            rms_to_hT(P, R, xT_d, gain_sb, gain_tok, hT[:, :, tsl], hT_toks[tt], t0, pb[0], pb_toks[0])
            for h in range(8):
                i = load_tile(0, h, t0)
                P.dve(lambda e, i=i, h=h, tsl=tsl: e.tensor_copy(out=yT[:, h, tsl], in_=ld[i][:]), reads=[ld_t[i]], writes=[yT_toks[tt][h]])
                i = load_tile(1, h, t0)
                head_rms_cm(P, ld[i][:], nrm[:], sg[:, 0:1], R["ones"], R["ones_tok"], sq, sq_t, pb[1], pb_toks[1], hr, hr_t,
                            ld_t[i], nrm_t, TT)
                i2 = load_tile(3, h, t0)
                P.act(lambda e, i2=i2: e.activation(out=sl[:], in_=ld[i2][:], func=AF.Silu), reads=[ld_t[i2]], writes=[sl_t])
                P.dve(lambda e, h=h, tsl=tsl: e.tensor_tensor(out=yT[:, 8 + h, tsl], in0=nrm[:], in1=sl[:], op=ALU.mult),
                      reads=[nrm_t, sl_t], writes=[yT_toks[tt][8 + h]])
                i = load_tile(2, h, t0)
                head_rms_cm(P, ld[i][:], nrm[:], sg[:, 1:2], R["ones"], R["ones_tok"], sq, sq_t, pb[1], pb_toks[1], hr, hr_t,
                            ld_t[i], nrm_t, TT)
                i2 = load_tile(4, h, t0)
                P.dve(lambda e, i2=i2, h=h: e.scalar_tensor_tensor(out=nrm[:], in0=ld[i2][:], scalar=sg[:, 2 + h:3 + h], in1=nrm[:],
                                                                   op0=ALU.mult, op1=ALU.add),
                      reads=[ld_t[i2], sg_tok, nrm_t], writes=[nrm_t])
                i3 = load_tile(5, h, t0)
                P.act(lambda e, i3=i3: e.activation(out=sl[:], in_=ld[i3][:], func=AF.Silu), reads=[ld_t[i3]], writes=[sl_t])
                P.dve(lambda e, h=h, tsl=tsl: e.tensor_tensor(out=yT[:, 16 + h, tsl], in0=nrm[:], in1=sl[:], op=ALU.mult),
                      reads=[nrm_t, sl_t], writes=[yT_toks[tt][16 + h]])
        for j in range(D // WCOLS):
            for i in range(3):
                iwi = load_w(wmg_d[i], j)
                ibi = load_b(wbr_d[i], j)
                for sub in range(WCOLS // 128):
                    cbk = j * (WCOLS // 128) + sub
                    for tt in range(NT):
                        tsl = slice(tt * TT, (tt + 1) * TT)
                        bg, bb = nextbank(), nextbank()
                        for c in range(KC):
                            P.pe(lambda e, iwi=iwi, bg=bg, c=c, sub=sub, tsl=tsl: e.matmul(
                                pb[bg][:], lhsT=wbuf[iwi][:, c, sub * 128:(sub + 1) * 128], rhs=hT[:, c, tsl],
                                start=(c == 0), stop=(c == KC - 1)), reads=[wtok[iwi], hT_toks[tt][c]], writes=[pb_toks[bg]])
                        for c in range(8):
                            P.pe(lambda e, ibi=ibi, bb=bb, c=c, sub=sub, i=i, tsl=tsl: e.matmul(
                                pb[bb][:], lhsT=bbuf[ibi][:, c, sub * 128:(sub + 1) * 128], rhs=yT[:, 8 * i + c, tsl],
                                start=(c == 0), stop=(c == 7)), reads=[btok[ibi], yT_toks[tt][8 * i + c]], writes=[pb_toks[bb]])
                        s = tt
                        a = sub * 2 + tt
                        P.act(lambda e, s=s, bg=bg: e.activation(out=sgm[s][:], in_=pb[bg][:], func=AF.Sigmoid),
                              reads=[pb_toks[bg]], writes=[sgm_t[s]])
                        if i == 0:
                            P.dve(lambda e, s=s, bb=bb, a=a: e.tensor_tensor(out=macc[a][:], in0=pb[bb][:], in1=sgm[s][:], op=ALU.mult),
                                  reads=[pb_toks[bb], sgm_t[s]], writes=[macc_t[a]])
                        else:
                            P.dve(lambda e, s=s, bb=bb: e.tensor_tensor(out=prod[s][:], in0=pb[bb][:], in1=sgm[s][:], op=ALU.mult),
                                  reads=[pb_toks[bb], sgm_t[s]], writes=[prod_t[s]])
                            if i == 1:
                                P.dve(lambda e, s=s, a=a: e.tensor_tensor(out=macc[a][:], in0=macc[a][:], in1=prod[s][:], op=ALU.add),
                                      reads=[macc_t[a], prod_t[s]], writes=[macc_t[a]])
                            else:
                                P.dve(lambda e, s=s, a=a: e.tensor_tensor(out=mst[s][:], in0=macc[a][:], in1=prod[s][:], op=ALU.add),
                                      reads=[macc_t[a], prod_t[s]], writes=[mst_t[s]])
                                P.dma("sp", scr_d[:, cbk, tsl], mst[s][:], reads=[mst_t[s]], writes=[scr_tok])
        for tt in range(NT):
            for q in range(4):
                P.dma("sp", hT[:, q * 8:(q + 1) * 8, tt * TT:(tt + 1) * TT], scr_d[:, q * 8:(q + 1) * 8, tt * TT:(tt + 1) * TT],
                      reads=[scr_tok], writes=hT_toks[tt][q * 8:(q + 1) * 8])
        ecount = 0
        for j in range(D // WCOLS):
            iwo = load_w(wout_d, j)
            for sub in range(WCOLS // 128):
                cbk = j * (WCOLS // 128) + sub
                for tt in range(NT):
                    t0 = tt * TT
                    bk = nextbank()
                    for c in range(KC):
                        P.pe(lambda e, iwo=iwo, bk=bk, c=c, sub=sub, tt=tt: e.matmul(
                            pb[bk][:], lhsT=wbuf[iwo][:, c, sub * 128:(sub + 1) * 128], rhs=hT[:, c, tt * TT:(tt + 1) * TT],
                            start=(c == 0), stop=(c == KC - 1)), reads=[wtok[iwo], hT_toks[tt][c]], writes=[pb_toks[bk]])
                    s = ecount % 2
                    ecount += 1
                    P.dma("sp", xr[s][:], xT_d[:, cbk, t0:t0 + TT], writes=[xr_toks[s]])
                    P.dve(lambda e, s=s, bk=bk: e.tensor_tensor(out=ot[s][:], in0=pb[bk][:], in1=xr[s][:], op=ALU.add),
                          reads=[pb_toks[bk], xr_toks[s]], writes=[ot_toks[s]])
                    P.dma("sp", oT_d[:, cbk, t0:t0 + TT], ot[s][:], reads=[ot_toks[s]])
        P.finish()
    return nc


def shard_cm(blocks):
    a = np.stack(blocks, 0)
    return [a[:, :, i * TOK:(i + 1) * TOK] for i in range(NCORES)]


def run_p4(xT_list, l, inp, oa, ob, hc, cm):
    nc = get_prog("p4", build_p4)
    g = fm_vec(inp["mix_norm"][l])
    smallg = np.ascontiguousarray(np.concatenate([inp["hg_out_gain"][l][:, None], inp["ml_out_gain"][l][:, None],
                                                  inp["ml_skip"][l].reshape(8, 128).T], axis=1))
    groups = [shard_cm([o.T for o in oa]), shard_cm([o.T for o in ob]), shard_cm([o.T for o in hc]),
              shard_cm([cm[48 + h] for h in range(8)]), shard_cm([cm[56 + h] for h in range(8)]),
              shard_cm([cm[72 + h] for h in range(8)])]
    wbr = np.stack([inp["w_branch_a"][l], inp["w_branch_b"][l], inp["w_branch_c"][l]], 0)
    in_maps = []
    for i in range(NCORES):
        br = np.ascontiguousarray(np.stack([grp[i] for grp in groups], 0))
        in_maps.append({"xT": xT_list[i], "gain": g, "br": br, "smallg": smallg, "wmg": inp["w_merge_gate"][l],
                        "wbr": wbr, "wout": inp["w_out"][l]})
    res = run_bass_kernel_spmd(nc, in_maps, core_ids=list(range(NCORES)))
    return [r["oT"] for r in res.results]


def kernel(**inp):
    inp = {k: np.asarray(v) for k, v in inp.items()}
    xT = to_fm(np.ascontiguousarray(inp["x"][0], dtype=np.float32))
    for l in range(DEPTH):
        xT = run_ffn(xT, inp["ffn1_norm"][l], inp["ffn1_w_gate"][l], inp["ffn1_w_up"][l], inp["ffn1_w_down"][l])
        cm, g2 = run_p2(xT, l, inp)
        oa = run_pa(cm, l, inp)
        ob = run_pb(cm, l, inp)
        hc = run_pc(cm, g2, l, inp)
        xT = run_p4(xT, l, inp, oa, ob, hc, cm)
        del cm, g2, oa, ob, hc
        xT = run_ffn(xT, inp["ffn2_norm"][l], inp["ffn2_w_gate"][l], inp["ffn2_w_up"][l], inp["ffn2_w_down"][l])
    return from_fm(xT).reshape(1, S, D).astype(np.float32)
```
